# Optimizing a Trainium2 kernel written in Bass

```python
import jax, jax.numpy as jnp
from jax import lax
import numpy as np

D_MODEL = 1024
BATCH = 8
SEQ = 2048
DEPTH = 2

GRID_W = 64
CTX_LEN = 256
HEAD_DIM = 64
ATTN_BLOCK = 128
WINDOW = 128
ROPE_THETA = 10000.0
NORM_EPS = 1e-6
N_MOD = 9
N_BRANCHES = 3
D_FF = 2816
A_HEADS = 8
A_KV_HEADS = 2
B_HEADS = 8
B_KV_HEADS = 2
C_HEADS = 8
C_NOPE_DIM = 64
C_ROPE_DIM = 32
C_QK_DIM = C_NOPE_DIM + C_ROPE_DIM
C_V_DIM = 64
C_Q_RANK = 384
C_KV_RANK = 256

A_Q = A_HEADS * HEAD_DIM
A_KV = A_KV_HEADS * HEAD_DIM
B_Q = B_HEADS * HEAD_DIM
B_KV = B_KV_HEADS * HEAD_DIM
C_OUT = C_HEADS * C_V_DIM
KV_SPLITS = (A_KV, A_KV, B_KV, B_KV, C_KV_RANK, C_ROPE_DIM)
Q_SPLITS = (A_Q, B_Q, C_Q_RANK)
GATE_COLS = N_BRANCHES * D_MODEL
KV_COLS = sum(KV_SPLITS)
IN_COLS = KV_COLS + sum(Q_SPLITS) + GATE_COLS

kernel_name = "hybrid_gated_tri_attention_dit_block"


def rms_norm(x, g):
    xf = x.astype(jnp.float32)
    y = xf * lax.rsqrt(jnp.mean(jnp.square(xf), axis=-1, keepdims=True) + NORM_EPS)
    return (y * g.astype(jnp.float32)).astype(x.dtype)


def adaln(h, g, shift, scale):
    return rms_norm(h, g) * (1 + scale) + shift


def swiglu(h, w_in, w_out):
    gate, up = jnp.split(h @ w_in, 2, axis=-1)
    return (jax.nn.silu(gate) * up) @ w_out


def split_cols(t, sizes):
    idx = [int(i) for i in np.cumsum(sizes)[:-1]]
    return jnp.split(t, idx, axis=-1)


def heads(t, n_heads):
    return t.reshape(t.shape[0], t.shape[1], n_heads, -1)


def axial_rope_tables(n_tokens, rot_dim):
    rows = n_tokens // GRID_W
    row = jnp.repeat(jnp.arange(rows, dtype=jnp.float32), GRID_W)
    col = jnp.tile(jnp.arange(GRID_W, dtype=jnp.float32), rows)
    n_freq = rot_dim // 4
    inv_freq = ROPE_THETA ** (-jnp.arange(n_freq, dtype=jnp.float32) / n_freq)
    ang = jnp.concatenate([row[:, None] * inv_freq, col[:, None] * inv_freq], axis=-1)
    return jnp.cos(ang)[:, None, :], jnp.sin(ang)[:, None, :]


def apply_rope(x, rope):
    cos, sin = rope
    cos = cos.astype(x.dtype)
    sin = sin.astype(x.dtype)
    x1, x2 = x[..., 0::2], x[..., 1::2]
    return jnp.stack([x1 * cos - x2 * sin, x1 * sin + x2 * cos], axis=-1).reshape(x.shape)


def qk_prep(t, n_heads, g, rope):
    t = rms_norm(heads(t, n_heads), g)
    return t if rope is None else apply_rope(t, rope)


def rope_tail(t, n_rot, rope):
    if rope is None:
        return t
    return jnp.concatenate([t[..., :-n_rot], apply_rope(t[..., -n_rot:], rope)], axis=-1)


def mla_keys_values(c_kv, k_rope, p, rope):
    kv = heads(rms_norm(c_kv, p["c_kv_lat_norm"]) @ p["c_w_ukv"], C_HEADS)
    k_nope, v = kv[..., :C_NOPE_DIM], kv[..., C_NOPE_DIM:]
    k_rope = jnp.broadcast_to(k_rope[:, :, None, :], k_nope.shape[:-1] + (C_ROPE_DIM,))
    k = rms_norm(jnp.concatenate([k_nope, k_rope], axis=-1), p["c_k_norm"])
    return rope_tail(k, C_ROPE_DIM, rope), v


def mla_queries(c_q, p, rope):
    q = heads(rms_norm(c_q, p["c_q_lat_norm"]) @ p["c_w_uq"], C_HEADS)
    return rope_tail(rms_norm(q, p["c_q_norm"]), C_ROPE_DIM, rope)


def blocked_gqa(q, k, v, sink=None):
    bsz, n_q, n_heads, dqk = q.shape
    n_kv = k.shape[2]
    grp = n_heads // n_kv
    dv = v.shape[-1]
    nb = n_q // ATTN_BLOCK
    scale = dqk ** -0.5
    qb = jnp.moveaxis(q.reshape(bsz, nb, ATTN_BLOCK, n_kv, grp, dqk), 1, 0)

    def attend(q_blk):
        s = jnp.einsum('bqkgd,bmkd->bkgqm', q_blk, k).astype(jnp.float32) * scale
        if sink is not None:
            s_sink = jnp.broadcast_to(sink.astype(jnp.float32).reshape(n_kv, grp, 1, 1), s.shape[:-1] + (1,))
            s = jnp.concatenate([s, s_sink], axis=-1)
        p = jax.nn.softmax(s, axis=-1).astype(v.dtype)
        if sink is not None:
            p = p[..., :-1]
        return jnp.einsum('bkgqm,bmkd->bqkgd', p, v)

    o = lax.map(attend, qb)
    return jnp.moveaxis(o, 0, 1).reshape(bsz, n_q, n_heads * dv)


def windowed_gqa_with_sink(q, k, v, k_ctx, v_ctx, sink):
    bsz, seq, n_heads, dh = q.shape
    n_kv = k.shape[2]
    grp = n_heads // n_kv
    n_ctx = k_ctx.shape[1]
    nb = seq // ATTN_BLOCK
    halo = (WINDOW // ATTN_BLOCK) * ATTN_BLOCK
    span = ATTN_BLOCK + 2 * halo
    scale = dh ** -0.5
    pad = ((0, 0), (halo, halo), (0, 0), (0, 0))
    kp = jnp.pad(k, pad)
    vp = jnp.pad(v, pad)
    qb = jnp.moveaxis(q.reshape(bsz, nb, ATTN_BLOCK, n_kv, grp, dh), 1, 0)
    offs = jnp.arange(span) - halo
    rel = offs[None, :] - jnp.arange(ATTN_BLOCK)[:, None]
    sink_f = sink.astype(jnp.float32).reshape(n_kv, grp, 1, 1)

    def attend(args):
        n, q_blk = args
        start = n * ATTN_BLOCK
        k_blk = lax.dynamic_slice_in_dim(kp, start, span, axis=1)
        v_blk = lax.dynamic_slice_in_dim(vp, start, span, axis=1)
        k_pos = start + offs
        valid = (jnp.abs(rel) <= WINDOW) & ((k_pos >= 0) & (k_pos < seq))[None, :]
        s_loc = jnp.einsum('bqkgd,bmkd->bkgqm', q_blk, k_blk).astype(jnp.float32) * scale
        s_loc = jnp.where(valid, s_loc, -jnp.inf)
        s_ctx = jnp.einsum('bqkgd,bmkd->bkgqm', q_blk, k_ctx).astype(jnp.float32) * scale
        s_sink = jnp.broadcast_to(sink_f, s_ctx.shape[:-1] + (1,))
        p = jax.nn.softmax(jnp.concatenate([s_loc, s_ctx, s_sink], axis=-1), axis=-1).astype(v.dtype)
        return (jnp.einsum('bkgqm,bmkd->bqkgd', p[..., :span], v_blk)
                + jnp.einsum('bkgqm,bmkd->bqkgd', p[..., span:span + n_ctx], v_ctx))

    o = lax.map(attend, (jnp.arange(nb), qb))
    return jnp.moveaxis(o, 0, 1).reshape(bsz, seq, n_heads * dh)


def merge_branches(ya, yb, yc, gates, p):
    ga, gb, gc = jnp.split(gates, N_BRANCHES, axis=-1)
    m = (jax.nn.sigmoid(ga) * (ya @ p["w_branch_a"])
         + jax.nn.sigmoid(gb) * (yb @ p["w_branch_b"])
         + jax.nn.sigmoid(gc) * (yc @ p["w_branch_c"]))
    return m @ p["w_out"]


def token_mixers(ux, uz, p, rope_hd, rope_c, ctx_out):
    w_in = p["w_in"]
    ak, av, bk, bv, ckv, ckr = split_cols(uz @ w_in[:, :KV_COLS], KV_SPLITS)
    ka_z = qk_prep(ak, A_KV_HEADS, p["a_k_norm"], None)
    va_z = heads(av, A_KV_HEADS)
    kb_z = qk_prep(bk, B_KV_HEADS, p["b_k_norm"], None)
    vb_z = heads(bv, B_KV_HEADS)
    kc_z, vc_z = mla_keys_values(ckv, ckr, p, None)
    ak, av, bk, bv, ckv, ckr, aq, bq, cq, gates = split_cols(ux @ w_in, KV_SPLITS + Q_SPLITS + (GATE_COLS,))
    ka_x = qk_prep(ak, A_KV_HEADS, p["a_k_norm"], rope_hd)
    va_x = heads(av, A_KV_HEADS)
    kb_x = qk_prep(bk, B_KV_HEADS, p["b_k_norm"], rope_hd)
    vb_x = heads(bv, B_KV_HEADS)
    kc_x, vc_x = mla_keys_values(ckv, ckr, p, rope_c)
    qa_x = qk_prep(aq, A_HEADS, p["a_q_norm"], rope_hd)
    qb_x = qk_prep(bq, B_HEADS, p["b_q_norm"], rope_hd)
    qc_x = mla_queries(cq, p, rope_c)
    ya = windowed_gqa_with_sink(qa_x, ka_x, va_x, ka_z, va_z, p["a_sink"])
    yb = blocked_gqa(qb_x, jnp.concatenate([kb_z, kb_x], axis=1), jnp.concatenate([vb_z, vb_x], axis=1))
    yc = blocked_gqa(qc_x, jnp.concatenate([kc_z, kc_x], axis=1), jnp.concatenate([vc_z, vc_x], axis=1))
    mix_x = merge_branches(ya, yb, yc, gates, p)
    if not ctx_out:
        return mix_x, None
    aq, bq, cq, gates = split_cols(uz @ w_in[:, KV_COLS:], Q_SPLITS + (GATE_COLS,))
    ya = blocked_gqa(qk_prep(aq, A_HEADS, p["a_q_norm"], None), ka_z, va_z, sink=p["a_sink"])
    yb = blocked_gqa(qk_prep(bq, B_HEADS, p["b_q_norm"], None), kb_z, vb_z)
    yc = blocked_gqa(mla_queries(cq, p, None), kc_z, vc_z)
    return mix_x, merge_branches(ya, yb, yc, gates, p)


def setup_inputs(seed: int = 0) -> dict:
    key = jax.random.key(seed)
    ks = jax.random.split(key, 32)
    f32 = jnp.float32

    def w(k, shape, fan_in, mult=1.0):
        return jax.random.normal(k, shape, f32) * (mult * fan_in ** -0.5)

    def gain(k, shape):
        return 1.0 + 0.02 * jax.random.normal(k, shape, f32)

    D = D_MODEL
    return {
        "x": jax.random.normal(ks[0], (BATCH, SEQ, D), f32),
        "c": jax.random.normal(ks[1], (BATCH, D), f32),
        "ctx": jax.random.normal(ks[2], (BATCH, CTX_LEN, D), f32),
        "c_ctx": jax.random.normal(ks[3], (D,), f32),
        "ada_w": w(ks[4], (DEPTH, D, N_MOD * D), D, 0.5),
        "ada_b": 0.02 * jax.random.normal(ks[5], (DEPTH, N_MOD * D), f32),
        "ffn1_norm": gain(ks[6], (DEPTH, D)),
        "ffn1_w_in": w(ks[7], (DEPTH, D, 2 * D_FF), D),
        "ffn1_w_out": w(ks[8], (DEPTH, D_FF, D), D_FF),
        "mix_norm": gain(ks[9], (DEPTH, D)),
        "mix_w_in": w(ks[10], (DEPTH, D, IN_COLS), D),
        "a_q_norm": gain(ks[11], (DEPTH, HEAD_DIM)),
        "a_k_norm": gain(ks[12], (DEPTH, HEAD_DIM)),
        "a_sink": 0.5 * jax.random.normal(ks[13], (DEPTH, A_HEADS), f32),
        "b_q_norm": gain(ks[14], (DEPTH, HEAD_DIM)),
        "b_k_norm": gain(ks[15], (DEPTH, HEAD_DIM)),
        "c_q_lat_norm": gain(ks[16], (DEPTH, C_Q_RANK)),
        "c_w_uq": w(ks[17], (DEPTH, C_Q_RANK, C_HEADS * C_QK_DIM), C_Q_RANK),
        "c_kv_lat_norm": gain(ks[18], (DEPTH, C_KV_RANK)),
        "c_w_ukv": w(ks[19], (DEPTH, C_KV_RANK, C_HEADS * (C_NOPE_DIM + C_V_DIM)), C_KV_RANK),
        "c_q_norm": gain(ks[20], (DEPTH, C_QK_DIM)),
        "c_k_norm": gain(ks[21], (DEPTH, C_QK_DIM)),
        "w_branch_a": w(ks[22], (DEPTH, A_Q, D), A_Q),
        "w_branch_b": w(ks[23], (DEPTH, B_Q, D), B_Q),
        "w_branch_c": w(ks[24], (DEPTH, C_OUT, D), C_OUT),
        "mix_w_out": w(ks[25], (DEPTH, D, D), D),
        "ffn2_norm": gain(ks[26], (DEPTH, D)),
        "ffn2_w_in": w(ks[27], (DEPTH, D, 2 * D_FF), D),
        "ffn2_w_out": w(ks[28], (DEPTH, D_FF, D), D_FF),
    }


def reference(x, c, ctx, c_ctx, ada_w, ada_b, ffn1_norm, ffn1_w_in, ffn1_w_out, mix_norm, mix_w_in,
              a_q_norm, a_k_norm, a_sink, b_q_norm, b_k_norm, c_q_lat_norm, c_w_uq, c_kv_lat_norm, c_w_ukv,
              c_q_norm, c_k_norm, w_branch_a, w_branch_b, w_branch_c, mix_w_out, ffn2_norm, ffn2_w_in,
              ffn2_w_out):
    n_lat = x.shape[1]
    rope_hd = axial_rope_tables(n_lat, HEAD_DIM)
    rope_c = axial_rope_tables(n_lat, C_ROPE_DIM)
    cond_x = jax.nn.silu(c)
    cond_z = jax.nn.silu(c_ctx)
    z = ctx
    for l in range(DEPTH):
        last = l == DEPTH - 1
        mx = [m[:, None, :] for m in jnp.split(cond_x @ ada_w[l] + ada_b[l], N_MOD, axis=-1)]
        mz = [m[None, None, :] for m in jnp.split(cond_z @ ada_w[l] + ada_b[l], N_MOD, axis=-1)]
        x = x + 0.5 * mx[2] * swiglu(adaln(x, ffn1_norm[l], mx[0], mx[1]), ffn1_w_in[l], ffn1_w_out[l])
        z = z + 0.5 * mz[2] * swiglu(adaln(z, ffn1_norm[l], mz[0], mz[1]), ffn1_w_in[l], ffn1_w_out[l])
        lp = {
            "w_in": mix_w_in[l], "a_q_norm": a_q_norm[l], "a_k_norm": a_k_norm[l], "a_sink": a_sink[l],
            "b_q_norm": b_q_norm[l], "b_k_norm": b_k_norm[l], "c_q_lat_norm": c_q_lat_norm[l],
            "c_w_uq": c_w_uq[l], "c_kv_lat_norm": c_kv_lat_norm[l], "c_w_ukv": c_w_ukv[l],
            "c_q_norm": c_q_norm[l], "c_k_norm": c_k_norm[l], "w_branch_a": w_branch_a[l],
            "w_branch_b": w_branch_b[l], "w_branch_c": w_branch_c[l], "w_out": mix_w_out[l],
        }
        mix_x, mix_z = token_mixers(adaln(x, mix_norm[l], mx[3], mx[4]), adaln(z, mix_norm[l], mz[3], mz[4]),
                                    lp, rope_hd, rope_c, not last)
        x = x + mx[5] * mix_x
        x = x + 0.5 * mx[8] * swiglu(adaln(x, ffn2_norm[l], mx[6], mx[7]), ffn2_w_in[l], ffn2_w_out[l])
        if not last:
            z = z + mz[5] * mix_z
            z = z + 0.5 * mz[8] * swiglu(adaln(z, ffn2_norm[l], mz[6], mz[7]), ffn2_w_in[l], ffn2_w_out[l])
    return x
```

```python
import contextlib
import numpy as np
import concourse.bass as bass
import concourse.mybir as mybir
from concourse.bass_utils import run_bass_kernel_spmd

F32 = mybir.dt.float32
BF16 = mybir.dt.bfloat16
AF = mybir.ActivationFunctionType
ALU = mybir.AluOpType

D = 1024
KC = 8
S = 2048
L = 256
T = S + L
DFF = 2816
NCH = T // 128
HB = 2
NHB = (DFF // 128) // HB
EPS = 1e-6
NCORES = 8
BLOB_C = 2048
TILES_ALL = [(0, 512), (512, 512), (1024, 512), (1536, 512), (2048, 256)]
TILES_X = TILES_ALL[:4]


def _pm(w, kc):
    n = w.shape[1]
    return np.ascontiguousarray(w.reshape(kc, 128, n).transpose(1, 0, 2))


def _rope_tables(rot_dim):
    n_tokens = S
    grid_w = 64
    row = np.repeat(np.arange(n_tokens // grid_w, dtype=np.float32), grid_w)
    col = np.tile(np.arange(grid_w, dtype=np.float32), n_tokens // grid_w)
    n_freq = rot_dim // 4
    inv_freq = (np.float32(10000.0) ** (-np.arange(n_freq, dtype=np.float32) / np.float32(n_freq))).astype(np.float32)
    ang = np.concatenate([row[:, None] * inv_freq, col[:, None] * inv_freq], axis=-1).astype(np.float32)
    return np.cos(ang).astype(np.float32), np.sin(ang).astype(np.float32)


class Blob:
    def __init__(self):
        self.parts = [[], []]
        self.off = [0, 0]
        self.index = {}
        self.g = 0

    def add(self, name, arr):
        arr = np.ascontiguousarray(arr, dtype=np.float32)
        assert arr.shape[0] == 128, (name, arr.shape)
        arr = arr.reshape(128, -1)
        g = self.g
        self.index[name] = (g, self.off[g], arr.shape[1])
        self.parts[g].append(arr.reshape(-1))
        self.off[g] += arr.size

    def finish(self):
        unit = NCORES * BLOB_C
        outs = []
        for g in range(2):
            tot = ((self.off[g] + unit - 1) // unit) * unit
            flat = np.zeros(tot, np.float32)
            o = 0
            for p in self.parts[g]:
                flat[o:o + p.size] = p
                o += p.size
            outs.append(flat.reshape(-1, BLOB_C))
        return outs


def build_blob(inp):
    b = Blob()
    cos64, sin64 = _rope_tables(64)
    p = np.arange(128)
    b.add("cos64", cos64[:, (p % 64) // 2].T)
    b.add("sin64", sin64[:, (p % 64) // 2].T)
    cos32, sin32 = _rope_tables(32)
    cC = np.ones((128, S), np.float32)
    sC = np.zeros((128, S), np.float32)
    pr = np.arange(64, 96)
    cC[64:96] = cos32[:, (pr - 64) // 2].T
    sC[64:96] = sin32[:, (pr - 64) // 2].T
    b.add("cosC", cC)
    b.add("sinC", sC)
    r64 = np.zeros((128, 128), np.float32)
    for i in range(64):
        r64[2 * i + 1, 2 * i] = -1.0
        r64[2 * i, 2 * i + 1] = 1.0
    rC = np.zeros((128, 128), np.float32)
    for i in range(16):
        rC[64 + 2 * i + 1, 64 + 2 * i] = -1.0
        rC[64 + 2 * i, 64 + 2 * i + 1] = 1.0
    bd = np.zeros((128, 128), np.float32)
    bd[:64, :64] = 1.0
    bd[64:, 64:] = 1.0
    mats = np.stack([r64, rC, np.ones((128, 128), np.float32), bd], axis=1)
    a = np.arange(128)[:, None]
    bq = np.arange(128)[None, :]
    masks = np.zeros((128, 6, 512), np.float32)
    for ri, rel in enumerate(range(-1, 5)):
        for blk in range(4):
            dlt = rel - blk
            if dlt == 0:
                m = np.ones((128, 128), np.float32)
            elif dlt == -1:
                m = (a >= bq).astype(np.float32)
            elif dlt == 1:
                m = (a <= bq).astype(np.float32)
            else:
                m = np.zeros((128, 128), np.float32)
            masks[:, ri, blk * 128:(blk + 1) * 128] = m
    b.add("cmats", np.concatenate([mats.reshape(128, -1), masks.reshape(128, -1)], axis=1))

    for l in range(2):
        b.g = l
        wa = _pm(inp["ada_w"][l], 8)
        for j in range(8):
            b.add(f"ada{l}_{j}", wa[:, :, j * 1152:(j + 1) * 1152])
        for fi, nm in ((1, "ffn1"), (2, "ffn2")):
            wi = _pm(inp[f"{nm}_w_in"][l], 8)
            wo = inp[f"{nm}_w_out"][l]
            for hb in range(NHB):
                c0 = hb * HB * 128
                gu = np.concatenate([wi[:, :, c0:c0 + HB * 128], wi[:, :, DFF + c0:DFF + c0 + HB * 128]], axis=2)
                wob = _pm(wo[c0:c0 + HB * 128], HB)
                b.add(f"{nm}_{l}_{hb}", np.concatenate([gu.reshape(128, -1), wob.reshape(128, -1)], axis=1))
        wm = _pm(inp["mix_w_in"][l], 8)
        def qperm(q):
            q = q.reshape(128, 8, 8, 64)
            order = [h for qc in range(4) for h in (qc, qc + 4)]
            return q[:, :, order, :].reshape(128, 8, 512)
        b.add(f"wA{l}", np.concatenate([wm[:, :, 0:128], wm[:, :, 128:256], qperm(wm[:, :, 800:1312])], axis=2))
        b.add(f"wB{l}", np.concatenate([wm[:, :, 256:384], wm[:, :, 384:512], qperm(wm[:, :, 1312:1824])], axis=2))
        b.add(f"wC{l}", np.concatenate([wm[:, :, 512:768], wm[:, :, 768:800], wm[:, :, 1824:2208]], axis=2))
        ukv = _pm(inp["c_w_ukv"][l], 2).reshape(128, 2, 8, 128)
        b.add(f"wukv{l}", np.concatenate([ukv[:, :, :, 0:64].reshape(128, 2, 512), ukv[:, :, :, 64:128].reshape(128, 2, 512)], axis=2))
        b.add(f"wuq{l}", _pm(inp["c_w_uq"][l], 3))
        wbr = [_pm(inp[f"w_branch_{x}"][l], 4) for x in "abc"]
        gcols = wm[:, :, 2208:5280]
        mr = np.zeros((128, 8, 4608), np.float32)
        for o in range(8):
            for br in range(3):
                mr[:, o, br * 512:(br + 1) * 512] = wbr[br][:, :, o * 128:(o + 1) * 128].reshape(128, 512)
                mr[:, o, 1536 + br * 1024:1536 + (br + 1) * 1024] = \
                    gcols[:, :, br * 1024 + o * 128: br * 1024 + (o + 1) * 128].reshape(128, 1024)
        for o in range(8):
            b.add(f"mrg{l}_{o}", mr[:, o])
        b.add(f"wout{l}", _pm(inp["mix_w_out"][l], 8))
        def colv(v, n):
            return np.ascontiguousarray(v.reshape(n, 128).T)
        def pad96(v):
            o_ = np.zeros((128, 1), np.float32)
            o_[:96, 0] = v
            return o_
        vec = np.concatenate([
            colv(inp["ada_b"][l], 72),
            colv(inp["ffn1_norm"][l], 8),
            colv(inp["mix_norm"][l], 8),
            colv(inp["ffn2_norm"][l], 8),
            np.tile(inp["a_q_norm"][l], 2)[:, None],
            np.tile(inp["a_k_norm"][l], 2)[:, None],
            np.tile(inp["b_q_norm"][l], 2)[:, None],
            np.tile(inp["b_k_norm"][l], 2)[:, None],
            colv(inp["c_q_lat_norm"][l], 3),
            colv(inp["c_kv_lat_norm"][l], 2),
            pad96(inp["c_q_norm"][l]),
            pad96(inp["c_k_norm"][l]),
            np.tile(inp["a_sink"][l][None, :], (128, 1)),
            np.zeros((128, 13), np.float32),
        ], axis=1).astype(np.float32)
        b.add(f"vec{l}", vec)
    return b


class KB:
    def __init__(self, nc, es):
        self.nc = nc
        self.es = es
        self.eng = {"pe": nc.tensor, "act": nc.scalar, "dve": nc.vector, "pool": nc.gpsimd, "sp": nc.sync}
        self.sem = {}
        self.cnt = {}
        for e in ("pe", "act", "dve", "pool"):
            self._src(e)
        self.dq = {"sp": [], "pool": []}
        for q, n in (("sp", 6), ("pool", 6)):
            for i in range(n):
                nm = f"d{q}{i}"
                self._src(nm)
                self.dq[q].append(nm)
        self._src("cc0")
        self._src("cc1")
        self.dqi = {"sp": 0, "pool": 0}
        self.seen = {e: {} for e in self.eng}
        self.res = {}
        self.snap = {}
        self.nins = 0
        self.log = {e: [] for e in self.eng}

    def _src(self, name):
        self.sem[name] = self.es.enter_context(self.nc.semaphore("s_" + name))
        self.cnt[name] = 0

    def _deps(self, r, w):
        deps = []
        for k in r:
            st = self.res.get(k)
            if st is not None and st[0] is not None:
                deps.append(st[0])
        for k in w:
            st = self.res.get(k)
            if st is not None:
                if st[0] is not None:
                    deps.append(st[0])
                deps.extend(st[1])
        return deps

    def _wait(self, e, deps):
        seen = self.seen[e]
        for (src, c) in deps:
            if src == e and (e == "pe" or c > self.cnt[e]):
                continue
            if seen.get(src, 0) >= c:
                continue
            self.eng[e].wait_ge(self.sem[src], c)
            self.log[e].append(("w", src, c))
            seen[src] = c
            sn = self.snap.get((src, c))
            if sn:
                for k2, v2 in sn.items():
                    if seen.get(k2, 0) < v2:
                        seen[k2] = v2

    def _mark(self, stamp, r, w):
        for k in r:
            st = self.res.get(k)
            if st is None:
                self.res[k] = [None, [stamp]]
            else:
                st[1].append(stamp)
                if len(st[1]) > 24:
                    best = {}
                    for (s_, c_) in st[1]:
                        if best.get(s_, 0) < c_:
                            best[s_] = c_
                    st[1] = list(best.items())
        for k in w:
            self.res[k] = [stamp, []]

    def op(self, e, fn, r=(), w=(), inc=True):
        self._wait(e, self._deps(r, w))
        ins = fn(self.eng[e])
        self.nins += 1
        if inc:
            self.cnt[e] += 1
            ins.then_inc(self.sem[e], 1)
            self.log[e].append(("i", e, 1))
            stamp = (e, self.cnt[e])
            sn = dict(self.seen[e])
            sn[e] = self.cnt[e]
            self.snap[stamp] = sn
            self.seen[e][e] = max(self.seen[e].get(e, 0), 0)
        else:
            stamp = (e, self.cnt[e] + 1)
        self._mark(stamp, r, w)
        return ins

    def dma(self, q, out, in_, r=(), w=()):
        i = self.dqi[q]
        self.dqi[q] = (i + 1) % len(self.dq[q])
        src = self.dq[q][i]
        deps = self._deps(r, w)
        deps.append((src, self.cnt[src]))
        self._wait(q, deps)
        ins = self.eng[q].dma_start(out=out, in_=in_)
        self.nins += 1
        self.cnt[src] += 16
        ins.then_inc(self.sem[src], 16)
        self.log[q].append(("i", src, 16))
        stamp = (src, self.cnt[src])
        self.snap[stamp] = dict(self.seen[q])
        self._mark(stamp, r, w)
        return stamp

    def barrier(self):
        for e in self.eng:
            deps = [(s_, c_) for s_, c_ in self.cnt.items() if c_ > 0]
            self._wait(e, deps)
        self.res = {}
        self.snap = {}


def build_nc(index, rows_shard, debug=None):
    nc = bass.Bass("TRN2", target_bir_lowering=False)
    xT_in = nc.dram_tensor("xT", [128, KC * T], F32, kind="ExternalInput")
    cT_in = nc.dram_tensor("cT", [128, 16], F32, kind="ExternalInput")
    wsh_in = [nc.dram_tensor(f"wsh{g}", [rows_shard[g], BLOB_C], F32, kind="ExternalInput") for g in range(2)]
    out_t = nc.dram_tensor("outT", [128, KC * S], F32, kind="ExternalOutput")
    wbounce = [nc.dram_tensor(f"wbounce{g}", [rows_shard[g], BLOB_C], F32) for g in range(2)]
    wfull = [nc.dram_tensor(f"wfull{g}", [NCORES * rows_shard[g], BLOB_C], F32) for g in range(2)]
    xscr = nc.dram_tensor("xscr", [128, KC * T], F32)
    yscr = nc.dram_tensor("yscr", [128, 12 * T], BF16)
    mscr = nc.dram_tensor("mscr", [128, KC * T], BF16)

    es = contextlib.ExitStack()
    with es:
        kb = KB(nc, es)

        def wblk(name, lo=0, n=None):
            g, off, f = index[name]
            if n is None:
                n = f - lo
            return bass.AP(wfull[g], off + lo, [[f, 128], [1, n]])

        for g in range(2):
            kb.dma("pool", wbounce[g][:, :], wsh_in[g][:, :], r=(), w=(("wbounce", g),))
        for g in range(2):
            kb._wait("pool", kb._deps((("wbounce", g),), (("wfull", g),)))
            ins = nc.gpsimd.collective_compute("AllGather", ALU.bypass, replica_groups=[list(range(NCORES))],
                                               ins=[wbounce[g].ap().opt()], outs=[wfull[g].ap().opt()])
            kb.cnt[f"cc{g}"] += 1
            ins.then_inc(kb.sem[f"cc{g}"], 1)
            kb.log["pool"].append(("i", f"cc{g}", 1))
            kb.snap[(f"cc{g}", 1)] = {}
            kb._mark((f"cc{g}", 1), (("wbounce", g),), (("wfull", g),))
        WF = (("wfull", 0), ("wfull", 1))
        kb.barrier()

        sbn = [0]

        def sb(name, shape, dt, stack=es):
            sbn[0] += 1
            return stack.enter_context(nc.sbuf_tensor(f"{name}_{sbn[0]}", shape, dt))

        cm = sb("cmats", [128, 4 * 128 + 6 * 512], BF16)
        vecs = [sb(f"vec{l}", [128, 128], F32) for l in range(2)]
        mod = [sb(f"mod{l}", [128, 72, 2], F32) for l in range(2)]
        cTs = sb("cTs", [128, 16], F32)
        condb = sb("condb", [128, 16], BF16)
        coef = sb("coef", [128, 64], F32)
        u = sb("u", [128, KC, T], BF16)
        epsc = sb("epsc", [128, 1], F32)
        ps = [es.enter_context(nc.psum_tensor(f"ps{i}", [128, 512], F32)) for i in range(8)]
        psi = {"a": 0, "b": 0, "c": 0}
        pools = {"a": [0, 1, 2, 3], "b": [4, 5], "c": [6, 7]}

        def psum(pool="a"):
            lst = pools[pool]
            i = lst[psi[pool] % len(lst)]
            psi[pool] += 1
            return ps[i], ("ps", i)

        R64 = cm[:, 0:128]
        RC = cm[:, 128:256]
        ONES = cm[:, 256:384]
        BD64 = cm[:, 384:512]

        def mask(ri):
            return cm[:, 512 + ri * 512: 512 + (ri + 1) * 512]

        kb.dma("pool", cm[:, :], wblk("cmats"), r=WF, w=("cm",))
        for l in range(2):
            kb.dma("sp", vecs[l][:, :], wblk(f"vec{l}"), r=WF, w=(("vec", l),))
        kb.dma("sp", cTs[:, :], cT_in[:, :], w=("cTs",))
        kb.op("act", lambda e: e.activation(out=condb[:, :], in_=cTs[:, :], func=AF.Silu), r=("cTs",), w=("condb",))

        kb.op("dve", lambda e: e.memset(epsc[:, :], float(EPS)), w=("epsc",))
        tmpn = [0]

        with contextlib.ExitStack() as ph:
            wada = [sb(f"wada{i}", [128, KC, 1152], BF16, ph) for i in range(2)]
            for l in range(2):
                pm_t, pm_k = psum("c")
                for j in range(8):
                    wt = wada[j % 2]
                    wk_ = ("wada", j % 2)
                    kb.dma("pool", wt[:, :, :].rearrange("p k n -> p (k n)"), wblk(f"ada{l}_{j}"), r=WF, w=(wk_,))
                    for fcl in range(9):
                        fc = j * 9 + fcl
                        for kc in range(KC):
                            last = (kc == KC - 1) and (fc == 71)
                            kb.op("pe", lambda e, wt=wt, kc=kc, fcl=fcl, fc=fc: e.matmul(
                                pm_t[:, fc * 2:fc * 2 + 2], wt[:, kc, fcl * 128:(fcl + 1) * 128],
                                condb[:, kc * 2:kc * 2 + 2], start=(kc == 0), stop=(kc == KC - 1)),
                                r=(wk_, "condb"), w=(pm_k,), inc=(kc == KC - 1 and fcl == 8))
                adab = vecs[l][:, 0:72]
                kb.op("dve", lambda e, l=l, adab=adab: e.tensor_tensor(
                    out=mod[l][:, :, :], in0=pm_t[:, 0:144].rearrange("p (f s) -> p f s", s=2),
                    in1=adab.unsqueeze(2).broadcast_to([128, 72, 2]), op=ALU.add),
                    r=(pm_k, ("vec", l)), w=(("mod", l),))
            kb.barrier()

        def modcol(l, m, c, s):
            return mod[l][:, m * 8 + c, s:s + 1]

        def compute_coef(l, gcol0, m_shift, m_scale, m_gate, gate_mul):
            g = vecs[l][:, gcol0:gcol0 + 8]
            for s in range(2):
                sc = mod[l][:, m_scale * 8:(m_scale + 1) * 8, s]
                sh = mod[l][:, m_shift * 8:(m_shift + 1) * 8, s]
                gt = mod[l][:, m_gate * 8:(m_gate + 1) * 8, s]
                kb.op("dve", lambda e, sc=sc, s=s: e.scalar_tensor_tensor(
                    out=coef[:, s * 8:(s + 1) * 8], in0=sc, scalar=1.0, in1=g, op0=ALU.add, op1=ALU.mult),
                    r=(("mod", l), ("vec", l)), w=("coef",))
                kb.op("dve", lambda e, sh=sh, s=s: e.tensor_copy(out=coef[:, 16 + s * 8:16 + (s + 1) * 8], in_=sh),
                      r=(("mod", l),), w=("coef",))
                kb.op("dve", lambda e, gt=gt, s=s: e.tensor_scalar(
                    out=coef[:, 32 + s * 8:32 + (s + 1) * 8], in0=gt, scalar1=float(gate_mul), scalar2=None, op0=ALU.mult),
                    r=(("mod", l),), w=("coef",))

        def rstd_from_ps(ss_t, ss_k, np_, w, inv_n, tmp, tmp_k):
            kb.op("act", lambda e: e.activation(out=tmp[:np_, :w], in_=ss_t[:np_, :w], func=AF.Sqrt,
                                                scale=float(inv_n), bias=epsc[:np_, 0:1]), r=(ss_k, "epsc"), w=(tmp_k,))
            kb.op("dve", lambda e: e.reciprocal(out=tmp[:np_, :w], in_=tmp[:np_, :w]), r=(tmp_k,), w=(tmp_k,))

        def adaln(xres, tiles, wk):
            for (t0, w) in tiles:
                s = 0 if t0 < S else 1
                sq, sqk = wk["sq"], "sq"
                kb.op("act", lambda e: e.activation(out=sq[:, :, :w], in_=xres[:, :, t0:t0 + w], func=AF.Square),
                      r=("x",), w=(sqk,))
                ss_t, ss_k = psum("c")
                for c in range(KC):
                    kb.op("pe", lambda e, c=c: e.matmul(ss_t[:, :w], ONES, sq[:, c, :w], start=(c == 0), stop=(c == KC - 1)),
                          r=(sqk, "cm"), w=(ss_k,), inc=(c == KC - 1))
                rs, rsk = wk["rs"], "rs"
                rstd_from_ps(ss_t, ss_k, 128, w, 1.0 / D, rs, rsk)
                for c in range(KC):
                    tb = wk["tf"][c % 2]
                    tbk = ("tf", c % 2)
                    kb.op("dve", lambda e, c=c, tb=tb: e.tensor_tensor(out=tb[:, :w], in0=xres[:, c, t0:t0 + w],
                                                                        in1=rs[:, :w], op=ALU.mult),
                          r=("x", rsk), w=(tbk,))
                    kb.op("act", lambda e, c=c, tb=tb: e.activation(
                        out=u[:, c, t0:t0 + w], in_=tb[:, :w], func=AF.Identity,
                        scale=coef[:, s * 8 + c:s * 8 + c + 1], bias=coef[:, 16 + s * 8 + c:16 + s * 8 + c + 1]),
                        r=(tbk, "coef"), w=(("u", t0),))

        def ffn(l, fi, xres, tiles, wk):
            nm = f"ffn{fi}"
            if fi == 1:
                compute_coef(l, 72, 0, 1, 2, 0.5)
            else:
                compute_coef(l, 88, 6, 7, 8, 0.5)
            adaln(xres, tiles, wk)
            wf = wk["wf"]
            for hb in range(NHB):
                wt = wf[hb % 2]
                wtk = ("wf", hb % 2)
                kb.dma("pool", wt[:, :], wblk(f"{nm}_{l}_{hb}"), r=WF, w=(wtk,))
                gu = wt[:, 0:KC * 2 * HB * 128].rearrange("p (k n) -> p k n", k=KC)
                wo = wt[:, KC * 2 * HB * 128:].rearrange("p (k n) -> p k n", k=HB)
                for (t0, w) in tiles:
                    s = 0 if t0 < S else 1
                    hT = wk["hT"][tmpn[0] % 2]
                    hTk = ("hT", tmpn[0] % 2)
                    tmpn[0] += 1
                    for hc in range(HB):
                        pg, pgk = psum("a")
                        pu, puk = psum("a")
                        for kc in range(KC):
                            kb.op("pe", lambda e, kc=kc, hc=hc, pg=pg: e.matmul(
                                pg[:, :w], gu[:, kc, hc * 128:(hc + 1) * 128], u[:, kc, t0:t0 + w],
                                start=(kc == 0), stop=(kc == KC - 1)), r=(wtk, ("u", t0)), w=(pgk,), inc=(kc == KC - 1))
                        for kc in range(KC):
                            kb.op("pe", lambda e, kc=kc, hc=hc, pu=pu: e.matmul(
                                pu[:, :w], gu[:, kc, HB * 128 + hc * 128:HB * 128 + (hc + 1) * 128], u[:, kc, t0:t0 + w],
                                start=(kc == 0), stop=(kc == KC - 1)), r=(wtk, ("u", t0)), w=(puk,), inc=(kc == KC - 1))
                        sg = wk["tf"][hc % 2]
                        sgk = ("tf", hc % 2)
                        kb.op("act", lambda e, pg=pg, sg=sg: e.activation(out=sg[:, :w], in_=pg[:, :w], func=AF.Silu),
                              r=(pgk,), w=(sgk,))
                        kb.op("dve", lambda e, pu=pu, sg=sg, hc=hc, hT=hT: e.tensor_tensor(
                            out=hT[:, hc, :w], in0=sg[:, :w], in1=pu[:, :w], op=ALU.mult), r=(sgk, puk), w=(hTk,))
                    for o in range(KC):
                        po, pok = psum("a")
                        for hc in range(HB):
                            kb.op("pe", lambda e, hc=hc, o=o, po=po, hT=hT: e.matmul(
                                po[:, :w], wo[:, hc, o * 128:(o + 1) * 128], hT[:, hc, :w],
                                start=(hc == 0), stop=(hc == HB - 1)), r=(wtk, hTk), w=(pok,), inc=(hc == HB - 1))
                        kb.op("dve", lambda e, o=o, po=po: e.scalar_tensor_tensor(
                            out=xres[:, o, t0:t0 + w], in0=po[:, :w], scalar=coef[:, 32 + s * 8 + o:32 + s * 8 + o + 1],
                            in1=xres[:, o, t0:t0 + w], op0=ALU.mult, op1=ALU.add), r=(pok, "coef", "x"), w=("x",))

        qn = [0]

        def qk_finish(src_t, src_k, np_, w, t0, gcol, inv_n, bdm, rmat, cos_t, sin_t, dst, dst_k, wk, rope):
            si = qn[0] % len(wk["qset"])
            qn[0] += 1
            qs = wk["qset"][si]
            sqh, sqhk = qs["sqh"], ("sqh", si)
            kb.op("act", lambda e: e.activation(out=sqh[:np_, :w], in_=src_t[:np_, :w], func=AF.Square, scale=gcol),
                  r=(src_k,), w=(sqhk,))
            ss_t, ss_k = psum("c")
            kb.op("pe", lambda e: e.matmul(ss_t[:np_, :w], bdm, sqh[:np_, :w], start=True, stop=True),
                  r=(sqhk, "cm"), w=(ss_k,))
            rs, rsk = qs["rs"], ("qrs", si)
            rstd_from_ps(ss_t, ss_k, np_, w, inv_n, rs, rsk)
            if not rope:
                kb.op("dve", lambda e: e.scalar_tensor_tensor(out=dst, in0=src_t[:np_, :w], scalar=gcol, in1=rs[:np_, :w],
                                                               op0=ALU.mult, op1=ALU.mult), r=(src_k, rsk), w=(dst_k,))
                return
            knb, knbk = qs["knb"], ("knb", si)
            kb.op("dve", lambda e: e.scalar_tensor_tensor(out=knb[:np_, :w], in0=src_t[:np_, :w], scalar=gcol,
                                                           in1=rs[:np_, :w], op0=ALU.mult, op1=ALU.mult),
                  r=(src_k, rsk), w=(knbk,))
            rt, rtk = psum("c")
            kb.op("pe", lambda e: e.matmul(rt[:np_, :w], rmat, knb[:np_, :w], start=True, stop=True),
                  r=(knbk, "cm"), w=(rtk,))
            t1, t1k = qs["t1"], ("qt1", si)
            t2, t2k = qs["t2"], ("qt2", si)
            kb.op("pool", lambda e: e.tensor_tensor(out=t1[:np_, :w], in0=knb[:np_, :w], in1=cos_t[:np_, t0:t0 + w], op=ALU.mult),
                  r=(knbk, "tab"), w=(t1k,))
            kb.op("dve", lambda e: e.tensor_tensor(out=t2[:np_, :w], in0=rt[:np_, :w], in1=sin_t[:np_, t0:t0 + w], op=ALU.mult),
                  r=(rtk, "tab"), w=(t2k,))
            kb.op("pool", lambda e: e.tensor_tensor(out=dst, in0=t1[:np_, :w], in1=t2[:np_, :w], op=ALU.add),
                  r=(t1k, t2k), w=(dst_k,))

        def attn_core(w, qfn, kfn, vfn, chunks, scale, maskfn, sink_col, ytile, ytk, h, wk, obase, qkey="Q"):
            ot, otk = psum("b")
            n = len(chunks)
            pend = None

            def do_pv(item, idx):
                pt, ptk, j = item
                kb.op("pe", lambda e: e.matmul(ot[:, :w], vfn(j), pt[:, :w], start=(idx == 0), stop=(idx == n - 1)),
                      r=(ptk, "V"), w=(otk,), inc=(idx == n - 1))

            for idx, j in enumerate(chunks):
                st, stk = psum("a")
                kb.op("pe", lambda e, j=j, st=st: e.matmul(st[:, :w], kfn(j), qfn(), start=True, stop=True),
                      r=("K", qkey), w=(stk,))
                pt = wk["pt"][tmpn[0] % 4]
                ptk = ("pt", tmpn[0] % 4)
                tmpn[0] += 1
                kb.op("act", lambda e, st=st, pt=pt: e.activation(out=pt[:, :w], in_=st[:, :w], func=AF.Exp, scale=float(scale)),
                      r=(stk,), w=(ptk,))
                mk = maskfn(j) if maskfn is not None else None
                if mk is not None:
                    kb.op("pool", lambda e, pt=pt, mk=mk: e.tensor_tensor(out=pt[:, :w], in0=pt[:, :w], in1=mk[:, :w], op=ALU.mult),
                          r=(ptk, "cm"), w=(ptk,))
                if pend is not None:
                    do_pv(pend, idx - 1)
                pend = (pt, ptk, j)
            do_pv(pend, n - 1)
            rd, rdk = wk["rd"], "rd"
            ob = obase
            db = 64 - obase
            if sink_col is not None:
                kb.op("dve", lambda e: e.tensor_scalar(out=rd[ob:ob + 64, :w], in0=ot[db:db + 64, :w], scalar1=sink_col(db), scalar2=None,
                                                       op0=ALU.add), r=(otk, "es"), w=(rdk,))
                kb.op("dve", lambda e: e.reciprocal(out=rd[ob:ob + 64, :w], in_=rd[ob:ob + 64, :w]), r=(rdk,), w=(rdk,))
            else:
                kb.op("dve", lambda e: e.reciprocal(out=rd[ob:ob + 64, :w], in_=ot[db:db + 64, :w]), r=(otk,), w=(rdk,))
            po = (h % 2) * 64
            kb.op("dve", lambda e: e.tensor_tensor(out=ytile[po:po + 64, h // 2, :w], in0=ot[ob:ob + 64, :w], in1=rd[ob:ob + 64, :w],
                                                   op=ALU.mult), r=(otk, rdk), w=(ytk,))

        def branch_ab(l, br, tiles_q, wk, ph):
            wname = "wA" if br == 0 else "wB"
            gq = vecs[l][:, 96 + 2 * br:97 + 2 * br]
            gk = vecs[l][:, 97 + 2 * br:98 + 2 * br]
            wt = sb(f"w{br}", [128, KC, 768], BF16, ph)
            KT = sb(f"KT{br}", [128, T], BF16, ph)
            VA = sb(f"VA{br}", [128, NCH, 192], BF16, ph)
            kb.dma("pool", wt[:, :, :].rearrange("p k n -> p (k n)"), wblk(f"{wname}{l}"), r=WF, w=("wbr",))
            kb.op("pool", lambda e: e.memset(VA[:, :, 64:128], 1.0), w=("V",))
            cos_t, sin_t = wk["cos"], wk["sin"]
            for (t0, w) in TILES_ALL:
                pk, pkk = psum("a")
                for kc in range(KC):
                    kb.op("pe", lambda e, kc=kc: e.matmul(pk[:, :w], wt[:, kc, 0:128], u[:, kc, t0:t0 + w],
                                                          start=(kc == 0), stop=(kc == KC - 1)),
                          r=("wbr", ("u", t0)), w=(pkk,), inc=(kc == KC - 1))
                qk_finish(pk, pkk, 128, w, t0, gk, 1.0 / 64, BD64, R64, cos_t, sin_t, KT[:, t0:t0 + w], "K", wk, rope=(t0 < S))
            for j in range(NCH):
                pv, pvk = psum("a")
                for kc in range(KC):
                    kb.op("pe", lambda e, kc=kc, j=j: e.matmul(pv[:, 0:128], u[:, kc, j * 128:(j + 1) * 128], wt[:, kc, 128:256],
                                                               start=(kc == 0), stop=(kc == KC - 1)),
                          r=("wbr", ("u", (j // 4) * 512)), w=(pvk,), inc=(kc == KC - 1))
                kb.op("act", lambda e, j=j, pv=pv: e.copy(out=VA[:, j, 0:64], in_=pv[:, 0:64]), r=(pvk,), w=("V",))
                kb.op("act", lambda e, j=j, pv=pv: e.copy(out=VA[:, j, 128:192], in_=pv[:, 64:128]), r=(pvk,), w=("V",))
            if br == 0:
                es_t = wk["es"]
                kb.op("act", lambda e: e.activation(out=es_t[:, 0:8], in_=vecs[l][:, 107:115], func=AF.Exp),
                      r=(("vec", l),), w=("es",))
            for qi_, (t0, w) in enumerate(tiles_q):
                qT, qTk = wk["qT"][qi_ % 2], ("Q", qi_ % 2)
                for qc in range(4):
                    pq, pqk = psum("a")
                    for kc in range(KC):
                        kb.op("pe", lambda e, kc=kc, qc=qc: e.matmul(pq[:, :w], wt[:, kc, 256 + qc * 128:256 + (qc + 1) * 128],
                                                                     u[:, kc, t0:t0 + w], start=(kc == 0), stop=(kc == KC - 1)),
                              r=("wbr", ("u", t0)), w=(pqk,), inc=(kc == KC - 1))
                    qk_finish(pq, pqk, 128, w, t0, gq, 1.0 / 64, BD64, R64, cos_t, sin_t, qT[:, qc, :w], qTk, wk, rope=(t0 < S))
                yt = wk["yt"][tmpn[0] % 2]
                ytk = ("yt", tmpn[0] % 2)
                tmpn[0] += 1
                is_z = t0 >= S
                qt = t0 // 512
                if is_z:
                    chunks = [16, 17]
                elif br == 0:
                    chunks = [j for j in range(4 * qt - 1, 4 * qt + 5) if 0 <= j < 16] + [16, 17]
                else:
                    chunks = list(range(NCH))
                for h in range(8):
                    kh, qc = h // 4, h % 4
                    kbp = kh * 64

                    def vfn(j, kh=kh):
                        return VA[:, j, kh * 64:kh * 64 + 128]

                    maskfn = None
                    if br == 0 and not is_z:
                        def maskfn(j, qt=qt):
                            if j >= 16:
                                return None
                            return mask(j - 4 * qt + 1)
                    attn_core(w, lambda kbp=kbp, qc=qc: qT[kbp:kbp + 64, qc, :w],
                              lambda j, kbp=kbp: KT[kbp:kbp + 64, j * 128:(j + 1) * 128],
                              vfn, chunks, 0.125, maskfn,
                              ((lambda db, h=h: wk["es"][db:db + 64, h:h + 1]) if br == 0 else None), yt, ytk, h, wk, kh * 64, qTk)
                dstap = bass.AP(yscr, br * 4 * T + t0, [[12 * T, 128], [T, 4], [1, w]])
                kb.dma("sp", dstap, yt[:, :, :w], r=(ytk,), w=(("yscr", br, t0),))

        def branch_c(l, tiles_q, wk, ph):
            wt = sb("wC", [128, KC, 672], BF16, ph)
            wukv = sb("wukv", [128, 2, 1024], BF16, ph)
            wuq = sb("wuq", [128, 3, 768], BF16, ph)
            KTC = sb("KTC", [128, 8, T], BF16, ph)
            VC = sb("VC", [128, NCH, 768], BF16, ph)
            ckvn = sb("ckvn", [128, 2, 512], BF16, ph)
            cqn = sb("cqn", [128, 3, 512], BF16, ph)
            kgs = [sb("kg96", [128, 512], F32, ph) for _ in range(2)]
            kb.dma("pool", wt[:, :, :].rearrange("p k n -> p (k n)"), wblk(f"wC{l}"), r=WF, w=("wbr",))
            kb.dma("pool", wukv[:, :, :].rearrange("p k n -> p (k n)"), wblk(f"wukv{l}"), r=WF, w=("wbr2",))
            kb.dma("pool", wuq[:, :, :].rearrange("p k n -> p (k n)"), wblk(f"wuq{l}"), r=WF, w=("wbr3",))
            for i4 in range(4):
                kb.op("pool", lambda e, i4=i4: e.memset(VC[:, :, i4 * 192 + 64:i4 * 192 + 128], 1.0), w=("V",))
            cos_t, sin_t = wk["cos"], wk["sin"]
            gqn = vecs[l][:, 105:106]
            gkn = vecs[l][:, 106:107]

            def latent(t0, w, col0, nfc, glat0, dst, dstk):
                pss = []
                sq, sqk = wk["sq"], "sq"
                for fc in range(nfc):
                    pt_, ptk_ = psum("a")
                    for kc in range(KC):
                        kb.op("pe", lambda e, kc=kc, fc=fc, pt_=pt_: e.matmul(
                            pt_[:, :w], wt[:, kc, col0 + fc * 128:col0 + (fc + 1) * 128], u[:, kc, t0:t0 + w],
                            start=(kc == 0), stop=(kc == KC - 1)), r=("wbr", ("u", t0)), w=(ptk_,), inc=(kc == KC - 1))
                    kb.op("act", lambda e, fc=fc, pt_=pt_: e.activation(out=sq[:, fc, :w], in_=pt_[:, :w], func=AF.Square),
                          r=(ptk_,), w=(sqk,))
                    pss.append((pt_, ptk_))
                ss_t, ss_k = psum("c")
                for fc in range(nfc):
                    kb.op("pe", lambda e, fc=fc: e.matmul(ss_t[:, :w], ONES, sq[:, fc, :w], start=(fc == 0), stop=(fc == nfc - 1)),
                          r=(sqk, "cm"), w=(ss_k,), inc=(fc == nfc - 1))
                rs, rsk = wk["rs"], "rs"
                rstd_from_ps(ss_t, ss_k, 128, w, 1.0 / (nfc * 128), rs, rsk)
                for fc in range(nfc):
                    pt_, ptk_ = pss[fc]
                    kb.op("dve", lambda e, fc=fc, pt_=pt_: e.scalar_tensor_tensor(
                        out=dst[:, fc, :w], in0=pt_[:, :w], scalar=vecs[l][:, glat0 + fc:glat0 + fc + 1], in1=rs[:, :w],
                        op0=ALU.mult, op1=ALU.mult), r=(ptk_, rsk, ("vec", l)), w=(dstk,))

            for (t0, w) in TILES_ALL:
                latent(t0, w, 0, 2, 103, ckvn, "ckvn")
                pr_, prk = psum("a")
                for kc in range(KC):
                    kb.op("pe", lambda e, kc=kc: e.matmul(pr_[0:32, :w], wt[:, kc, 256:288], u[:, kc, t0:t0 + w],
                                                          start=(kc == 0), stop=(kc == KC - 1)),
                          r=("wbr", ("u", t0)), w=(prk,), inc=(kc == KC - 1))
                for ki in range(2):
                    kb.op("act", lambda e, ki=ki: e.copy(out=kgs[ki][64:96, :w], in_=pr_[0:32, :w]), r=(prk,), w=(("kg", ki),))
                for jj in range(w // 128):
                    j = t0 // 128 + jj
                    pv, pvk = psum("a")
                    for k2 in range(2):
                        kb.op("pe", lambda e, k2=k2, jj=jj: e.matmul(pv[:, :], ckvn[:, k2, jj * 128:(jj + 1) * 128], wukv[:, k2, 512:1024],
                                                                     start=(k2 == 0), stop=(k2 == 1)),
                              r=("wbr2", "ckvn"), w=(pvk,), inc=(k2 == 1))
                    for t2 in range(2):
                        kb.op("act", lambda e, j=j, pv=pv, t2=t2: e.copy(
                            out=bass.AP(VC, j * 768 + t2 * 128, [[NCH * 768, 128], [192, 4], [1, 64]]),
                            in_=pv[:, :].rearrange("p (i t d) -> p i t d", i=4, t=2)[:, :, t2, :]), r=(pvk,), w=("V",))
                for h in range(8):
                    pk, pkk = psum("a")
                    for k2 in range(2):
                        kb.op("pe", lambda e, k2=k2, h=h: e.matmul(pk[0:64, :w], wukv[:, k2, h * 64:(h + 1) * 64], ckvn[:, k2, :w],
                                                                   start=(k2 == 0), stop=(k2 == 1)),
                              r=("wbr2", "ckvn"), w=(pkk,), inc=(k2 == 1))
                    kg = kgs[h % 2]
                    kb.op("act", lambda e: e.copy(out=kg[0:64, :w], in_=pk[0:64, :w]), r=(pkk,), w=(("kg", h % 2),))
                    qk_finish(kg, ("kg", h % 2), 96, w, t0, gkn[0:96, :], 1.0 / 96, ONES[0:96, 0:96], RC[0:96, 0:96], cos_t, sin_t,
                              KTC[0:96, h, t0:t0 + w], "K", wk, rope=(t0 < S))
            for qi_, (t0, w) in enumerate(tiles_q):
                latent(t0, w, 288, 3, 100, cqn, "cqn")
                qT, qTk = wk["qTC"][qi_ % 2], ("Q", qi_ % 2)
                for h in range(8):
                    pq, pqk = psum("a")
                    for k3 in range(3):
                        kb.op("pe", lambda e, k3=k3, h=h: e.matmul(pq[0:96, :w], wuq[:, k3, h * 96:(h + 1) * 96], cqn[:, k3, :w],
                                                                   start=(k3 == 0), stop=(k3 == 2)),
                              r=("wbr3", "cqn"), w=(pqk,), inc=(k3 == 2))
                    qk_finish(pq, pqk, 96, w, t0, gqn[0:96, :], 1.0 / 96, ONES[0:96, 0:96], RC[0:96, 0:96], cos_t, sin_t,
                              qT[0:96, h, :w], qTk, wk, rope=(t0 < S))
                yt = wk["yt"][tmpn[0] % 2]
                ytk = ("yt", tmpn[0] % 2)
                tmpn[0] += 1
                chunks = [16, 17] if t0 >= S else list(range(NCH))
                for h in range(8):
                    def vfn(j, h=h):
                        c0 = (h // 2) * 192 + (h % 2) * 64
                        return VC[:, j, c0:c0 + 128]
                    attn_core(w, lambda h=h: qT[0:96, h, :w], lambda j, h=h: KTC[0:96, h, j * 128:(j + 1) * 128],
                              vfn, chunks, 96 ** -0.5, None, None, yt, ytk, h, wk, (h % 2) * 64, qTk)
                dstap = bass.AP(yscr, 2 * 4 * T + t0, [[12 * T, 128], [T, 4], [1, w]])
                kb.dma("sp", dstap, yt[:, :, :w], r=(ytk,), w=(("yscr", 2, t0),))

        def merge(l, tiles, ph, wk):
            mrg = sb("mrg", [128, 8, 4608], BF16, ph)
            ybuf = sb("ybuf", [128, 12, 512], BF16, ph)
            mts = [sb(f"mt{i}", [128, KC, 512], BF16, ph) for i in range(2)]
            for o in range(8):
                kb.dma("pool", mrg[:, o, :], wblk(f"mrg{l}_{o}"), r=WF, w=(("mrg", o),))
            for ti, (t0, w) in enumerate(tiles):
                mt = mts[ti % 2]
                mtk = ("mt", ti % 2)
                srcap = bass.AP(yscr, t0, [[12 * T, 128], [T, 12], [1, w]])
                kb.dma("sp", ybuf[:, :, :w], srcap, r=tuple(("yscr", b_, t0) for b_ in range(3)), w=("ybuf",))
                for o in range(8):
                    acc, acck = wk["acc"], "acc"
                    for br in range(3):
                        pp, ppk = psum("a")
                        for kc in range(4):
                            kb.op("pe", lambda e, kc=kc, br=br, o=o, pp=pp: e.matmul(
                                pp[:, :w], mrg[:, o, br * 512 + kc * 128:br * 512 + (kc + 1) * 128], ybuf[:, br * 4 + kc, :w],
                                start=(kc == 0), stop=(kc == 3)), r=(("mrg", o), "ybuf"), w=(ppk,), inc=(kc == 3))
                        pg, pgk = psum("a")
                        for kc in range(KC):
                            c0 = 1536 + br * 1024 + kc * 128
                            kb.op("pe", lambda e, kc=kc, c0=c0, o=o, pg=pg: e.matmul(
                                pg[:, :w], mrg[:, o, c0:c0 + 128], u[:, kc, t0:t0 + w], start=(kc == 0), stop=(kc == KC - 1)),
                                r=(("mrg", o), ("u", t0)), w=(pgk,), inc=(kc == KC - 1))
                        sg, sgk = wk["tf"][br % 2], ("tf", br % 2)
                        kb.op("act", lambda e, pg=pg, sg=sg: e.activation(out=sg[:, :w], in_=pg[:, :w], func=AF.Sigmoid),
                              r=(pgk,), w=(sgk,))
                        if br == 0:
                            kb.op("dve", lambda e, pp=pp, sg=sg: e.tensor_tensor(out=acc[:, :w], in0=sg[:, :w], in1=pp[:, :w], op=ALU.mult),
                                  r=(sgk, ppk), w=(acck,))
                        else:
                            kb.op("dve", lambda e, pp=pp, sg=sg: e.tensor_tensor(out=sg[:, :w], in0=sg[:, :w], in1=pp[:, :w], op=ALU.mult),
                                  r=(sgk, ppk), w=(sgk,))
                            if br == 1:
                                kb.op("pool", lambda e, sg=sg: e.tensor_tensor(out=acc[:, :w], in0=acc[:, :w], in1=sg[:, :w], op=ALU.add),
                                      r=(sgk, acck), w=(acck,))
                            else:
                                kb.op("pool", lambda e, sg=sg, o=o, mt=mt: e.tensor_tensor(out=mt[:, o, :w], in0=acc[:, :w], in1=sg[:, :w], op=ALU.add),
                                      r=(sgk, acck), w=(mtk,))
                dstap = bass.AP(mscr, t0, [[KC * T, 128], [T, KC], [1, w]])
                kb.dma("sp", dstap, mt[:, :, :w], r=(mtk,), w=(("mscr", t0),))

        def mix_out(l, tiles, xres, ph):
            wo = sb("wout", [128, KC, D], BF16, ph)
            mbs = [sb(f"mb{i}", [128, KC, 512], BF16, ph) for i in range(2)]
            kb.dma("pool", wo[:, :, :].rearrange("p k n -> p (k n)"), wblk(f"wout{l}"), r=WF, w=("wout",))
            for ti, (t0, w) in enumerate(tiles):
                s = 0 if t0 < S else 1
                mb = mbs[ti % 2]
                mbk = ("mb", ti % 2)
                srcap = bass.AP(mscr, t0, [[KC * T, 128], [T, KC], [1, w]])
                kb.dma("sp", mb[:, :, :w], srcap, r=(("mscr", t0),), w=(mbk,))
                for o in range(KC):
                    po, pok = psum("a")
                    for kc in range(KC):
                        kb.op("pe", lambda e, kc=kc, o=o, po=po, mb=mb: e.matmul(po[:, :w], wo[:, kc, o * 128:(o + 1) * 128], mb[:, kc, :w],
                                                                                 start=(kc == 0), stop=(kc == KC - 1)),
                              r=("wout", mbk), w=(pok,), inc=(kc == KC - 1))
                    kb.op("dve", lambda e, o=o, po=po, s=s: e.scalar_tensor_tensor(
                        out=xres[:, o, t0:t0 + w], in0=po[:, :w], scalar=mod[l][:, 5 * 8 + o, s:s + 1],
                        in1=xres[:, o, t0:t0 + w], op0=ALU.mult, op1=ALU.add), r=(pok, ("mod", l), "x"), w=("x",))

        dbg = debug or {}
        stop = dbg.get("stop")

        def alloc_work(ph, names, nq=3):
            wk = {}
            if "tf" in names:
                wk["tf"] = [sb(f"tf{i}", [128, 512], F32, ph) for i in range(2)]
            if "rs" in names:
                wk["rs"] = sb("rs", [128, 512], F32, ph)
            if "sq" in names:
                wk["sq"] = sb("sq", [128, KC, 512], BF16, ph)
            if "ffn" in names:
                wk["hT"] = [sb(f"hT{i}", [128, HB, 512], BF16, ph) for i in range(2)]
                wk["wf"] = [sb(f"wf{i}", [128, KC * 2 * HB * 128 + HB * D], BF16, ph) for i in range(2)]
            if "att" in names:
                wk["qset"] = [dict(sqh=sb("sqh", [128, 512], BF16, ph), knb=sb("knb", [128, 512], BF16, ph),
                                   rs=sb("qrs", [128, 512], F32, ph), t1=sb("qt1", [128, 512], F32, ph),
                                   t2=sb("qt2", [128, 512], F32, ph)) for _ in range(nq)]
                wk["pt"] = [sb(f"pt{i}", [128, 512], BF16, ph) for i in range(4)]
                wk["rd"] = sb("rd", [128, 512], F32, ph)
                wk["yt"] = [sb(f"yt{i}", [128, 4, 512], BF16, ph) for i in range(2)]
                wk["es"] = sb("es", [128, 8], F32, ph)
                wk["cos"] = sb("cos", [128, S], F32, ph)
                wk["sin"] = sb("sin", [128, S], F32, ph)
            if "acc" in names:
                wk["acc"] = sb("acc", [128, 512], F32, ph)
            return wk

        def load_x(xres, src):
            for c in range(KC):
                kb.dma("sp", xres[:, c, :], bass.AP(src, c * T, [[KC * T, 128], [1, T]]), r=("xsrc",), w=("x",))

        def spill_x(xres):
            for c in range(KC):
                kb.dma("sp", bass.AP(xscr, c * T, [[KC * T, 128], [1, T]]), xres[:, c, :], r=("x",), w=("xsrc",))

        def write_out(xres):
            for c in range(KC):
                kb.dma("sp", bass.AP(out_t, c * S, [[KC * S, 128], [1, S]]), xres[:, c, 0:S], r=("x",), w=("out",))
            kb.barrier()

        def x_span(l_mix, l_ffn1):
            pools["a"] = [0, 1, 2, 3, 4, 5]
            pools["c"] = [6, 7]
            _x_span_doc = """x-resident span: [mix_out(l_mix) + FFN2(l_mix)] then [FFN1(l_ffn1) + mixer adaLN(l_ffn1) + spill].
            returns True if the program is finished."""
            with contextlib.ExitStack() as ph:
                xres = sb("xres", [128, KC, T], F32, ph)
                wk = alloc_work(ph, ("tf", "rs", "sq"))
                if l_mix is None:
                    load_x(xres, xT_in)
                    if stop == "mods":
                        write_out(xres)
                        return True
                else:
                    load_x(xres, xscr)
                    last = (l_mix == 1)
                    tl = TILES_ALL
                    with contextlib.ExitStack() as ph2:
                        mix_out(l_mix, tl, xres, ph2)
                        kb.barrier()
                    if stop == f"mix{l_mix}":
                        write_out(xres)
                        return True
                    with contextlib.ExitStack() as ph2:
                        wk.update(alloc_work(ph2, ("ffn",)))
                        ffn(l_mix, 2, xres, tl, wk)
                        kb.barrier()
                    if last or stop == f"ffn2_{l_mix}":
                        write_out(xres)
                        return True
                with contextlib.ExitStack() as ph2:
                    wk.update(alloc_work(ph2, ("ffn",)))
                    ffn(l_ffn1, 1, xres, TILES_ALL, wk)
                    kb.barrier()
                if stop == f"ffn1_{l_ffn1}":
                    write_out(xres)
                    return True
                compute_coef(l_ffn1, 80, 3, 4, 5, 1.0)
                adaln(xres, TILES_ALL, wk)
                spill_x(xres)
                kb.barrier()
            return False

        def dump_x():
            with contextlib.ExitStack() as ph:
                xres = sb("xres", [128, KC, T], F32, ph)
                load_x(xres, xscr)
                write_out(xres)

        def mixer(l):
            last = (l == 1)
            tiles_q = TILES_ALL
            pools["a"] = [0, 1, 2]
            pools["b"] = [3, 4]
            pools["c"] = [5, 6, 7]
            for br in range(2):
                with contextlib.ExitStack() as ph:
                    wk = alloc_work(ph, ("att",))
                    wk["qT"] = [sb("qT", [128, 4, 512], BF16, ph) for _ in range(2)]
                    kb.dma("sp", wk["cos"][:, :], wblk("cos64"), r=WF, w=("tab",))
                    kb.dma("sp", wk["sin"][:, :], wblk("sin64"), r=WF, w=("tab",))
                    branch_ab(l, br, tiles_q, wk, ph)
                    kb.barrier()
                if stop == f"br{'AB'[br]}{l}":
                    dump_x()
                    return True
            with contextlib.ExitStack() as ph:
                wk = alloc_work(ph, ("rs", "att"), nq=2)
                wk["sq"] = sb("sq", [128, 3, 512], BF16, ph)
                wk["qTC"] = [sb("qTC", [128, 8, 512], BF16, ph) for _ in range(2)]
                kb.dma("sp", wk["cos"][:, :], wblk("cosC"), r=WF, w=("tab",))
                kb.dma("sp", wk["sin"][:, :], wblk("sinC"), r=WF, w=("tab",))
                branch_c(l, tiles_q, wk, ph)
                kb.barrier()
            if stop == f"brC{l}":
                dump_x()
                return True
            with contextlib.ExitStack() as ph:
                wk = alloc_work(ph, ("tf", "acc"))
                merge(l, tiles_q, ph, wk)
                kb.barrier()
            if stop == f"mrg{l}":
                dump_x()
                return True
            return False

        if dbg.get("only_l1"):
            done = x_span(None, 1)
        else:
            done = x_span(None, 0)
            if not done:
                done = mixer(0)
            if not done:
                done = x_span(0, 1)
        if not done:
            done = mixer(1)
        if not done:
            done = x_span(1, None)
        assert done
        print("[kernel] instructions:", kb.nins, {k: v for k, v in kb.cnt.items()})
        if dbg.get("simulate"):
            sems = {k: 0 for k in kb.cnt}
            pos = {e: 0 for e in kb.log}
            progress = True
            while progress:
                progress = False
                for e, lg in kb.log.items():
                    while pos[e] < len(lg):
                        kind, src, v = lg[pos[e]]
                        if kind == "w":
                            if sems[src] >= v:
                                pos[e] += 1
                                progress = True
                            else:
                                break
                        else:
                            sems[src] += v
                            pos[e] += 1
                            progress = True
            stuck = {e: (pos[e], len(lg), lg[pos[e]] if pos[e] < len(lg) else None) for e, lg in kb.log.items()}
            print("[simulate]", stuck, {k: (sems[k], kb.cnt[k]) for k in sems if sems[k] != kb.cnt[k]})
    return nc


_DEBUG = None


def kernel(**inputs):
    import time as _time
    _t0 = _time.time()
    inp = {k: np.asarray(v) for k, v in inputs.items()}
    blob = build_blob(inp)
    flats = blob.finish()
    rs = [f.shape[0] // NCORES for f in flats]
    print("[kernel] blob built", round(_time.time() - _t0, 1), flush=True)
    nc = build_nc(blob.index, rs, debug=_DEBUG)
    print("[kernel] program built", round(_time.time() - _t0, 1), flush=True)
    x = inp["x"]
    ctx = inp["ctx"]
    c = inp["c"]
    c_ctx = inp["c_ctx"]
    in_maps = []
    for b in range(NCORES):
        xz = np.concatenate([x[b], ctx[b]], axis=0)
        xT = np.ascontiguousarray(xz.reshape(T, KC, 128).transpose(2, 1, 0)).reshape(128, KC * T)
        cT = np.zeros((128, KC, 2), np.float32)
        cT[:, :, 0] = c[b].reshape(KC, 128).T
        cT[:, :, 1] = c_ctx.reshape(KC, 128).T
        in_maps.append({"xT": xT.astype(np.float32), "cT": cT.reshape(128, 16),
                        "wsh0": np.ascontiguousarray(flats[0][b * rs[0]:(b + 1) * rs[0]]),
                        "wsh1": np.ascontiguousarray(flats[1][b * rs[1]:(b + 1) * rs[1]])})
    print("[kernel] inputs laid out", round(_time.time() - _t0, 1), flush=True)
    if _DEBUG and _DEBUG.get("trace"):
        res = run_bass_kernel_spmd(nc, in_maps, core_ids=list(range(NCORES)), trace=True)
        print("[kernel] exec_time_ns", res.exec_time_ns, flush=True)
    else:
        res = run_bass_kernel_spmd(nc, in_maps, core_ids=list(range(NCORES)))
    print("[kernel] device run done", round(_time.time() - _t0, 1), flush=True)
    out = np.empty((NCORES, S, D), np.float32)
    for b in range(NCORES):
        oT = np.asarray(res.results[b]["outT"]).reshape(128, KC, S)
        out[b] = oT.transpose(2, 1, 0).reshape(S, D)
    return out
```
